# Optimizing a Trainium2 kernel written in Bass

```python
import jax, jax.numpy as jnp
from jax import lax
import numpy as np

D_MODEL = 1024
BATCH = 1
SEQ = 16384
DEPTH = 4

HEAD_DIM = 64
N_Q_HEADS = 12
N_KV_HEADS_A = 4
N_MEM_HEADS = 4
N_MEM = 256
MIX_WIDTH = (N_Q_HEADS + N_MEM_HEADS) * HEAD_DIM
W_IN_A = (N_Q_HEADS + 2 * N_KV_HEADS_A + N_MEM_HEADS) * HEAD_DIM
W_IN_B = (3 * N_Q_HEADS + N_MEM_HEADS) * HEAD_DIM
WINDOW = 128
BLOCK = 128
GRID_W = 64
NA_ROWS_MAX = 8
NA_COLS = 16
D_FF = -(-8 * D_MODEL // (3 * 256)) * 256
N_LAYERS_A = (DEPTH + 1) // 2
N_LAYERS_B = DEPTH // 2
EPS = 1e-6
NEG_INF = -1e30

kernel_name = "hybrid_window_gqa_neighbourhood_memory_encoder"


def rmsnorm(x, g):
    x32 = x.astype(jnp.float32)
    y = x32 * lax.rsqrt(jnp.mean(x32 * x32, axis=-1, keepdims=True) + EPS)
    return (y * g.astype(jnp.float32)).astype(x.dtype)


def alibi_slopes(n_heads):
    return 2.0 ** (-8.0 * jnp.arange(1, n_heads + 1, dtype=jnp.float32) / n_heads)


def windowed_gqa(q, k, v, sink):
    B, S, H, D = q.shape
    Hk = k.shape[2]
    G = H // Hk
    nb = S // BLOCK
    qb = q.reshape(B, nb, BLOCK, Hk, G, D)
    pad = ((0, 0), (BLOCK, BLOCK), (0, 0), (0, 0))
    kp = jnp.pad(k, pad).reshape(B, nb + 2, BLOCK, Hk, D)
    vp = jnp.pad(v, pad).reshape(B, nb + 2, BLOCK, Hk, D)
    kband = jnp.concatenate([kp[:, :-2], kp[:, 1:-1], kp[:, 2:]], axis=2)
    vband = jnp.concatenate([vp[:, :-2], vp[:, 1:-1], vp[:, 2:]], axis=2)
    q_pos = jnp.arange(S).reshape(nb, BLOCK)
    k_pos = jnp.arange(nb)[:, None] * BLOCK - BLOCK + jnp.arange(3 * BLOCK)[None, :]
    dist = jnp.abs(q_pos[:, :, None] - k_pos[:, None, :])
    allowed = (dist <= WINDOW) & (k_pos[:, None, :] >= 0) & (k_pos[:, None, :] < S)
    slopes = alibi_slopes(H).reshape(1, Hk, G, 1, 1, 1)
    s = jnp.einsum('bnikgd,bnukd->bkgniu', qb, kband).astype(jnp.float32) * (D ** -0.5)
    s = s - slopes * dist.astype(jnp.float32)[None, None, None]
    s = jnp.where(allowed[None, None, None], s, NEG_INF)
    sink_col = jnp.broadcast_to(sink.astype(jnp.float32).reshape(1, Hk, G, 1, 1, 1), s.shape[:-1] + (1,))
    p = jax.nn.softmax(jnp.concatenate([s, sink_col], axis=-1), axis=-1)[..., :-1]
    o = jnp.einsum('bkgniu,bnukd->bnikgd', p.astype(v.dtype), vband)
    return o.reshape(B, S, H * D)


def neighbourhood_attn(q, k, v, rpb):
    B, S, H, D = q.shape
    rows = S // GRID_W
    kr = min(NA_ROWS_MAX, rows)
    qg = q.reshape(B, rows, GRID_W, H, D)
    kg = k.reshape(B, rows, GRID_W, H, D)
    vg = v.reshape(B, rows, GRID_W, H, D)
    r = jnp.arange(rows)
    row_start = jnp.clip(r - kr // 2, 0, rows - kr)
    row_idx = row_start[:, None] + jnp.arange(kr)[None, :]
    kband = kg[:, row_idx]
    vband = vg[:, row_idx]
    c = jnp.arange(GRID_W)
    col_start = jnp.clip(c - NA_COLS // 2, 0, GRID_W - NA_COLS)
    in_win = (c[None, :] >= col_start[:, None]) & (c[None, :] < col_start[:, None] + NA_COLS)
    dr_idx = row_idx - r[:, None] + NA_ROWS_MAX - 1
    dc_idx = jnp.clip(c[None, :] - c[:, None], -(NA_COLS - 1), NA_COLS - 1) + NA_COLS - 1
    bias = rpb.astype(jnp.float32)[:, dr_idx][:, :, :, dc_idx]
    bias = bias.transpose(0, 1, 3, 2, 4)
    s = jnp.einsum('brchd,brijhd->bhrcij', qg, kband).astype(jnp.float32) * (D ** -0.5)
    s = s + bias[None]
    s = jnp.where(in_win[None, None, None, :, None, :], s, NEG_INF)
    sh = s.shape
    p = jax.nn.softmax(s.reshape(sh[:-2] + (kr * GRID_W,)), axis=-1).reshape(sh)
    o = jnp.einsum('bhrcij,brijhd->brchd', p.astype(v.dtype), vband)
    return o.reshape(B, S, H * D)


def memory_attn(q, k, v):
    B, S, Hm, D = q.shape
    s = jnp.einsum('bshd,bmhd->bhsm', q, k).astype(jnp.float32) * (D ** -0.5)
    p = jax.nn.softmax(s, axis=-1)
    o = jnp.einsum('bhsm,bmhd->bshd', p.astype(v.dtype), v)
    return o.reshape(B, S, Hm * D)


def setup_inputs(seed: int = 0) -> dict:
    key = jax.random.key(seed)
    ks = jax.random.split(key, 16)
    f32 = jnp.float32
    nrm = lambda k, shape, scale: jax.random.normal(k, shape, f32) * scale
    return {
        "x": nrm(ks[0], (BATCH, SEQ, D_MODEL), 1.0),
        "mem": nrm(ks[1], (BATCH, N_MEM, D_MODEL), 1.0),
        "norm_mix": 1.0 + nrm(ks[2], (DEPTH, D_MODEL), 0.02),
        "norm_ffn": 1.0 + nrm(ks[3], (DEPTH, D_MODEL), 0.02),
        "norm_mem": 1.0 + nrm(ks[4], (D_MODEL,), 0.02),
        "norm_final": 1.0 + nrm(ks[5], (D_MODEL,), 0.02),
        "w_in_a": nrm(ks[6], (N_LAYERS_A, D_MODEL, W_IN_A), D_MODEL ** -0.5),
        "sink_a": nrm(ks[7], (N_LAYERS_A, N_Q_HEADS), 0.5),
        "w_in_b": nrm(ks[8], (N_LAYERS_B, D_MODEL, W_IN_B), D_MODEL ** -0.5),
        "rpb_b": nrm(ks[9], (N_LAYERS_B, N_Q_HEADS, 2 * NA_ROWS_MAX - 1, 2 * NA_COLS - 1), 0.3),
        "w_mem_kv": nrm(ks[10], (DEPTH, D_MODEL, 2 * N_MEM_HEADS * HEAD_DIM), D_MODEL ** -0.5),
        "w_out": nrm(ks[11], (DEPTH, MIX_WIDTH, D_MODEL), MIX_WIDTH ** -0.5),
        "w_gate_up": nrm(ks[12], (DEPTH, D_MODEL, 2 * D_FF), D_MODEL ** -0.5),
        "w_down": nrm(ks[13], (DEPTH, D_FF, D_MODEL), D_FF ** -0.5),
    }


def reference(x, mem, norm_mix, norm_ffn, norm_mem, norm_final, w_in_a, sink_a, w_in_b, rpb_b,
              w_mem_kv, w_out, w_gate_up, w_down):
    B, S, _ = x.shape
    qw = N_Q_HEADS * HEAD_DIM
    kvw = N_KV_HEADS_A * HEAD_DIM
    mw = N_MEM_HEADS * HEAD_DIM
    mem_n = rmsnorm(mem, norm_mem)
    for i in range(DEPTH):
        li = i // 2
        h = rmsnorm(x, norm_mix[i])
        mkv = mem_n @ w_mem_kv[i]
        mk = mkv[..., :mw].reshape(B, N_MEM, N_MEM_HEADS, HEAD_DIM)
        mv = mkv[..., mw:].reshape(B, N_MEM, N_MEM_HEADS, HEAD_DIM)
        if i % 2 == 0:
            proj = h @ w_in_a[li]
            q, k, v, qm = jnp.split(proj, [qw, qw + kvw, qw + 2 * kvw], axis=-1)
            y_tok = windowed_gqa(q.reshape(B, S, N_Q_HEADS, HEAD_DIM),
                                 k.reshape(B, S, N_KV_HEADS_A, HEAD_DIM),
                                 v.reshape(B, S, N_KV_HEADS_A, HEAD_DIM), sink_a[li])
        else:
            proj = h @ w_in_b[li]
            q, k, v, qm = jnp.split(proj, [qw, 2 * qw, 3 * qw], axis=-1)
            y_tok = neighbourhood_attn(q.reshape(B, S, N_Q_HEADS, HEAD_DIM),
                                       k.reshape(B, S, N_Q_HEADS, HEAD_DIM),
                                       v.reshape(B, S, N_Q_HEADS, HEAD_DIM), rpb_b[li])
        y_mem = memory_attn(qm.reshape(B, S, N_MEM_HEADS, HEAD_DIM), mk, mv)
        x = x + jnp.concatenate([y_tok, y_mem], axis=-1) @ w_out[i]
        h = rmsnorm(x, norm_ffn[i])
        gu = h @ w_gate_up[i]
        x = x + (jax.nn.silu(gu[..., :D_FF]) * gu[..., D_FF:]) @ w_down[i]
    return rmsnorm(x, norm_final)
```

```python
import numpy as np
import concourse.bass as bass
import concourse.mybir as mybir
from concourse.bass_utils import run_bass_kernel_spmd

F32 = mybir.dt.float32
BF16 = mybir.dt.bfloat16
AF = mybir.ActivationFunctionType
ALU = mybir.AluOpType

NCORES = 8
D = 1024
SEQ = 16384
TOK_CORE = SEQ // NCORES
NBLK = 28
OWN_LO, OWN_HI = 6, 21
HALO_TOK = OWN_LO * 128
DFF = 2816
NFF = DFF // 128
EPS = 1e-6
NEG = -1.0e4
LAYERS = [("A", 1, 26, 0, 27), ("B", 3, 24, 1, 26), ("A", 4, 23, 3, 24), ("B", 6, 21, 4, 23)]
SPECIAL_B = (6, 7, 20, 21)
DBG = {}
SLOPES = [float(2.0 ** (-8.0 * (h + 1) / 12.0)) for h in range(12)]


class Sched:
    def __init__(self):
        self.ins = []

    def add(self, eng, fn, r=(), w=(), dsem=None):
        self.ins.append(dict(eng=eng, fn=fn, r=tuple(r), w=tuple(w), dsem=dsem, bar=False))

    def barrier(self):
        self.ins.append(dict(eng=None, bar=True))

    def finalize(self):
        ins = self.ins
        lastw, readers = {}, {}
        last_on_eng = {}
        pending_bar = None
        bar_seen = {}
        for i, I in enumerate(ins):
            if I["bar"]:
                pending_bar = dict(last_on_eng)
                bar_seen = {}
                I["deps"] = {}
                continue
            deps = {}
            for r in I["r"]:
                for wr in lastw.get(r, {}).values():
                    deps[wr] = True
            for w in I["w"]:
                for wr in lastw.get(w, {}).values():
                    deps.setdefault(wr, False)
                for rd in readers.get(w, {}).values():
                    deps.setdefault(rd, False)
            if pending_bar is not None and I["eng"] not in bar_seen:
                bar_seen[I["eng"]] = True
                for e, idx in pending_bar.items():
                    deps.setdefault(idx, True)
                I["after_bar"] = True
            deps.pop(i, None)
            I["deps"] = deps
            key = I["eng"] if I["dsem"] is None else ("dma", I["dsem"])
            for r in I["r"]:
                readers.setdefault(r, {})[key] = i
            for w in I["w"]:
                lastw.setdefault(w, {})[key] = i
            last_on_eng[key] = i
        prev_on_eng = {}
        last_c = {}
        for i, I in enumerate(ins):
            if I["bar"] or I["dsem"] is not None:
                continue
            prev_on_eng[i] = last_c.get(I["eng"])
            last_c[I["eng"]] = i
        for I in ins:
            I["marked"] = False
            I["spacer"] = False
        for i, I in enumerate(ins):
            if I["bar"]:
                continue
            keep = {}
            for d, raw in I["deps"].items():
                P = ins[d]
                if P["dsem"] is None and I["dsem"] is None and P["eng"] == I["eng"]:
                    if P["eng"] == "pe" or not DBG.get("selfsync", False):
                        if raw and P["eng"] != "pe" and prev_on_eng.get(i) == d:
                            I["spacer"] = True
                        continue
                keep[d] = raw
                if P["dsem"] is None:
                    P["marked"] = True
            I["deps"] = keep
        cnt = {}
        dcount = {}
        waited = {}
        for i, I in enumerate(ins):
            if I["bar"]:
                continue
            need = {}
            for d in I["deps"]:
                P = ins[d]
                if P["dsem"] is not None:
                    s = "d:" + P["dsem"]
                    v = 16 * dcount.get(P["dsem"], 0)
                else:
                    s = "e:" + P["eng"]
                    v = P["cnt"]
                if v > need.get(s, 0):
                    need[s] = v
            if I.get("after_bar"):
                for s_, c_ in dcount.items():
                    need["d:" + s_] = max(need.get("d:" + s_, 0), 16 * c_)
            wd = waited.setdefault(I["eng"], {})
            waits = []
            for s, v in need.items():
                if v > wd.get(s, 0):
                    wd[s] = v
                    waits.append((s, v))
            I["waits"] = waits
            if I["dsem"] is not None:
                dcount[I["dsem"]] = dcount.get(I["dsem"], 0) + 1
            elif I["marked"]:
                cnt[I["eng"]] = cnt.get(I["eng"], 0) + 1
                I["cnt"] = cnt[I["eng"]]
        semnames = set()
        for I in ins:
            if I["bar"]:
                continue
            for s, _ in I["waits"]:
                semnames.add(s)
            if I["dsem"] is not None:
                semnames.add("d:" + I["dsem"])
            elif I["marked"]:
                semnames.add("e:" + I["eng"])
        return sorted(semnames)

    def emit(self, eng_name, e, sems, spacer=None):
        for I in self.ins:
            if I["bar"] or I["eng"] != eng_name:
                continue
            for s, v in I["waits"]:
                e.wait_ge(sems[s], v)
            if I["spacer"] and spacer is not None:
                spacer(e)
            r = I["fn"](e)
            if I["dsem"] is not None:
                r.then_inc(sems["d:" + I["dsem"]], 16)
            elif I["marked"]:
                r.then_inc(sems["e:" + I["eng"]], 1)


def build_program(depth=4):
    nc = bass.Bass("TRN2", target_bir_lowering=False)
    S = Sched()

    def din(name, shape):
        return nc.dram_tensor(name, list(shape), F32, kind="ExternalInput").ap()

    xw_d = din("xw", [128, 8, NBLK * 128])
    memT_d = din("memT", [128, 8, 256])
    gvec_d = din("gvec", [128, 80])
    kb_d = din("kb", [128, NBLK])
    sinkb_d = din("sinkb", [128, 24])
    negd_d = din("negdA", [128, 3, 128])
    wkv_d, wqp_d, wop_d, wmk_d, wgu_d, wdn_d, tab_d = [], [], [], [], [], [], {}
    for l in range(depth):
        typ = LAYERS[l][0]
        wkv_d.append(din(f"wkv{l}", [128, 8, 512 if typ == "A" else 1536]))
        wqp_d.append(din(f"wqp{l}", [8, 128, 8, 128]))
        wop_d.append(din(f"wop{l}", [8, 128, 8, 128]))
        wmk_d.append(din(f"wmk{l}", [128, 8, 512]))
        wgu_d.append(din(f"wgu{l}", [NFF, 128, 8, 256]))
        wdn_d.append(din(f"wdn{l}", [8, 128, NFF, 128]))
        if typ == "B":
            tab_d[l] = din(f"tab{l}", [5, 12, 128, 6 * 128])
    out_d = nc.dram_tensor("out", [128, 8, TOK_CORE], F32, kind="ExternalOutput").ap()

    from contextlib import ExitStack
    es = ExitStack()
    NX = 26 * 128
    X = es.enter_context(nc.sbuf_tensor("X", [128, 8, NX], F32))
    onesn = es.enter_context(nc.sbuf_tensor("onesn", [128, 128], BF16))
    ones1 = es.enter_context(nc.sbuf_tensor("ones1", [128, 64], BF16))
    gvec = es.enter_context(nc.sbuf_tensor("gvecs", [128, 80], F32))
    kb = es.enter_context(nc.sbuf_tensor("kbs", [128, NBLK], F32))
    epsc = es.enter_context(nc.sbuf_tensor("epsc", [128, 1], F32))
    dumv = es.enter_context(nc.sbuf_tensor("dumv", [128, 64], F32))
    duma = es.enter_context(nc.sbuf_tensor("duma", [128, 64], F32))
    sinkb = es.enter_context(nc.sbuf_tensor("sinkbs", [128, 24], F32))
    kmT = es.enter_context(nc.sbuf_tensor("kmTs", [128, 2, 256], BF16))
    vma = es.enter_context(nc.sbuf_tensor("vma", [128, 2, 4, 128], BF16))
    RB = 98 * 1024
    R = es.enter_context(nc.sbuf_tensor("R", [128, RB // 2], BF16))
    PS = [es.enter_context(nc.psum_tensor(f"ps{k}", [128, 512], F32)) for k in range(8)]

    def carve(off, shape, dt):
        n = int(np.prod(shape[1:]))
        nbytes = n * (4 if dt == F32 else 2)
        assert off % 4 == 0 and off + nbytes <= RB, (off, nbytes)
        ap = R[:, off // 2:(off + nbytes) // 2]
        if dt == F32:
            ap = ap.bitcast(F32)
        if len(shape) == 3:
            ap = ap.rearrange("p (a b) -> p a b", a=shape[1])
        elif len(shape) == 4:
            ap = ap.rearrange("p (a b c) -> p a b c", a=shape[1], b=shape[2])
        return ap

    def pb(k, half=None):
        return [f"p{k}"]

    def xnames(bs):
        return [f"x{b}" for b in bs]

    S.add("sp", lambda e: e.dma_start(out=gvec[:], in_=gvec_d[:]), w=["gvec"], dsem="c")
    S.add("sp", lambda e: e.dma_start(out=kb[:], in_=kb_d[:]), w=["kb"], dsem="c")
    S.add("sp", lambda e: e.dma_start(out=sinkb[:], in_=sinkb_d[:]), w=["sinkb"], dsem="c")
    S.add("dve", lambda e: e.memset(onesn[:], 1.0 / 1024.0), w=["onesn"])
    S.add("dve", lambda e: e.memset(ones1[:], 1.0), w=["ones1"])
    S.add("dve", lambda e: e.memset(epsc[:], EPS), w=["epsc"])
    S.add("dve", lambda e: e.memset(duma[:], 0.0), w=["duma"])
    for c in range(8):
        S.add("sp", (lambda c: lambda e: e.dma_start(out=X[:, c, :], in_=xw_d[:, c, 128:128 + NX]))(c),
              w=[f"xc{c}"], dsem="x")

    def G(kind, l, c):
        base = {"mix": 0, "ffn": 32, "mem": 64, "fin": 72}[kind]
        idx = base + (l * 8 + c if kind in ("mix", "ffn") else c)
        return gvec[:, idx:idx + 1]

    def rstd_ops(out_ap, ps_ap, psnames, name):
        S.add("act", lambda e: e.activation(out=out_ap, in_=ps_ap, func=AF.Ln, bias=epsc[:, 0:1]), r=psnames + ["epsc"], w=[name])
        S.add("act", lambda e: e.activation(out=out_ap, in_=out_ap, func=AF.Exp, scale=-0.5), r=[name], w=[name])

    def rms_sq_mean(src_list, sq, sqname, ntok, psbank, pshalf):
        for ap, toff, n, names in src_list:
            S.add("act", (lambda ap, toff, n: lambda e: e.activation(out=sq[:, :, toff:toff + n], in_=ap, func=AF.Square))(ap, toff, n),
                  r=names, w=[sqname])
        ps = PS[psbank][:, pshalf * 256:pshalf * 256 + ntok] if ntok <= 256 else PS[psbank][:, 0:ntok]
        pn = pb(psbank, pshalf) if ntok <= 256 else pb(psbank)
        for c in range(8):
            S.add("pe", (lambda c: lambda e: e.matmul(ps, lhsT=onesn[:], rhs=sq[:, c, 0:ntok], start=(c == 0), stop=(c == 7)))(c),
                  r=[sqname, "onesn"], w=pn)
        return ps, pn

    def emit_layer(l):
        typ, olo, ohi, klo, khi = LAYERS[l]
        isA = typ == "A"
        li = l // 2
        KC = 2 if isA else 6
        VW = 256 if isA else 768
        RS = 6
        LA = 1
        NQ = LA + 1
        NPC = 16 if isA else 4

        off = 0

        def take(shape, dt):
            nonlocal off
            n = int(np.prod(shape[1:])) * (4 if dt == F32 else 2)
            ap = carve(off, shape, dt)
            off += (n + 3) // 4 * 4
            return ap

        def phase_a():
            nonlocal off
            S.barrier()
            off = 0
            WKV = take([128, 8, 512 if isA else 1536], BF16)
            WPC = take([128, NPC, 8, 128], BF16)
            ET = take([128, 12, 3 if isA else 6, 128], BF16)
            toff0 = off
            RK = take([128, RS, KC, 128], BF16)
            RV = take([128, RS, 4, 128], BF16) if isA else take([128, RS, 768], BF16)
            QT = take([128, NQ, 8, 256], BF16)
            HT = take([128, 8, 256], BF16)
            PBT = take([128, 2, 1152 if isA else 1024], BF16)
            toff1 = off
            YT = take([128, 8, 256], BF16)
            SQ = YT
            PFW = 384 if isA else 768
            NPF = 3 if isA else 2
            PF = take([128, NPF, PFW], F32)
            TABF = PF
            REC = take([128, 512], F32)
            RSTD = take([128, 256], F32)
            EXS = take([128, 12], F32)
            XE = take([128, 8, 256], F32) if l == 0 else None
            assert off <= RB, (l, off)
            off_save = off
            off = toff0
            MEMF = take([128, 8, 256], F32)
            MEMN = take([128, 8, 256], BF16)
            WMK = take([128, 8, 512], BF16)
            RSTDM = take([128, 256], F32)
            assert off <= toff1, (l, off, toff1)
            off = off_save

            def slot(b):
                return ((b // 2) % (RS // 2)) * 2 + (b % 2)

            S.add("pool", lambda e: e.dma_start(out=WKV, in_=wkv_d[l][:]), w=["wkv"], dsem="wkv")
            S.add("pool", lambda e: e.dma_start(out=WMK, in_=wmk_d[l][:]), w=["wmk"], dsem="wkv")
            S.add("sp", lambda e: e.dma_start(out=MEMF, in_=memT_d[:]), w=["memf"], dsem="t")
            if l == 0:
                S.add("sp", lambda e: e.dma_start(out=XE[:, :, 0:128], in_=xw_d[:, :, 0:128]), w=["x0"], dsem="t")
                S.add("sp", lambda e: e.dma_start(out=XE[:, :, 128:256], in_=xw_d[:, :, 27 * 128:28 * 128]), w=["x27"], dsem="t")
            if isA:
                for j in range(16):
                    src = wqp_d[l][j] if j < 8 else wop_d[l][j - 8]
                    S.add("pool", (lambda j, src: lambda e: e.dma_start(out=WPC[:, j], in_=src))(j, src), w=[f"wp{j}"], dsem="wres")
            pc_ctr = [0]

            def piece(kind, j):
                if isA:
                    s = j if kind == "q" else 8 + j
                    return WPC[:, s], f"wp{s}"
                s = pc_ctr[0] % NPC
                pc_ctr[0] += 1
                src = wqp_d[l][j] if kind == "q" else wop_d[l][j]
                S.add("pool", (lambda s, src: lambda e: e.dma_start(out=WPC[:, s], in_=src))(s, src), w=[f"wp{s}"], dsem=f"wp{s}")
                return WPC[:, s], f"wp{s}"

            if isA:
                S.add("sp", lambda e: e.dma_start(out=TABF[:, 0, 0:384], in_=negd_d[:].rearrange("p a b -> p (a b)")), w=["pf0"], dsem="t")
                for h in range(12):
                    S.add("act", (lambda h: lambda e: e.activation(out=ET[:, h].rearrange("p a b -> p (a b)"), in_=TABF[:, 0, 0:384],
                                                                    func=AF.Exp, scale=SLOPES[h]))(h), r=["pf0"], w=["et"])
                S.add("act", lambda e: e.activation(out=EXS[:, :], in_=sinkb[:, li * 12:(li + 1) * 12], func=AF.Exp), r=["sinkb"], w=["exs"])
            tab_state = [None]
            tab_ctr = [0]

            def ensure_table(kind):
                if tab_state[0] == kind:
                    return
                tab_state[0] = kind
                for h in range(12):
                    s = tab_ctr[0] % 2
                    tab_ctr[0] += 1
                    S.add("sp", (lambda h, s: lambda e: e.dma_start(out=TABF[:, s, :], in_=tab_d[l][kind, h]))(h, s), w=[f"pf{s}"], dsem=f"tb{s}")
                    S.add("act", (lambda h, s: lambda e: e.activation(out=ET[:, h].rearrange("p a b -> p (a b)"), in_=TABF[:, s, :], func=AF.Exp))(h, s),
                          r=[f"pf{s}"], w=["et"])

            psm, pnm = rms_sq_mean([(MEMF, 0, 256, ["memf"])], SQ, "yt", 256, 0, 0)
            rstd_ops(RSTDM, psm, pnm, "rstdm")
            for c in range(8):
                S.add("dve", (lambda c: lambda e: e.scalar_tensor_tensor(out=MEMN[:, c, :], in0=MEMF[:, c, :], scalar=G("mem", 0, c), in1=RSTDM,
                                                                          op0=ALU.mult, op1=ALU.mult))(c), r=["memf", "rstdm", "gvec"], w=["memn"])
            for j in range(2):
                for c in range(8):
                    S.add("pe", (lambda j, c: lambda e: e.matmul(PS[1][:, j * 256:(j + 1) * 256], lhsT=WMK[:, c, j * 128:(j + 1) * 128], rhs=MEMN[:, c, :],
                                                                 start=(c == 0), stop=(c == 7)))(j, c), r=["wmk", "memn"], w=pb(1, j))
                S.add("dve", (lambda j: lambda e: e.tensor_copy(out=kmT[:, j, :], in_=PS[1][:, j * 256:(j + 1) * 256]))(j), r=pb(1, j), w=["kmT"])
            S.add("dve", lambda e: e.memset(vma[:, :, :, 64:128], 1.0), w=["vma"])
            for mc in range(2):
                for c in range(8):
                    S.add("pe", (lambda mc, c: lambda e: e.matmul(PS[2][:, mc * 256:(mc + 1) * 256], lhsT=MEMN[:, c, mc * 128:(mc + 1) * 128], rhs=WMK[:, c, 256:512],
                                                                  start=(c == 0), stop=(c == 7)))(mc, c), r=["wmk", "memn"], w=pb(2, mc))
                S.add("dve", (lambda mc: lambda e: e.tensor_copy(out=vma[:, mc, :, 0:64],
                                                                 in_=PS[2][:, mc * 256:(mc + 1) * 256].rearrange("p (a b) -> p a b", a=4)))(mc), r=pb(2, mc), w=["vma"])
            S.barrier()
            if isA:
                S.add("dve", lambda e: e.memset(RV[:, :, :, 64:128], 1.0), w=[f"rv{s}" for s in range(RS)])

            tiles = []
            b = klo
            while b <= khi:
                if b % 2 == 1:
                    tiles.append([b])
                    b += 1
                else:
                    tiles.append([bb for bb in (b, b + 1) if bb <= khi])
                    b += 2

            def xsrc(bs):
                segs = []
                run = []
                for bb in bs:
                    if 1 <= bb <= 26:
                        run.append(bb)
                    else:
                        segs.append((XE[:, :, (0 if bb == 0 else 128):(128 if bb == 0 else 256)], (bb - bs[0]) * 128, 128, [f"x{bb}"]))
                if run:
                    segs.append((X[:, :, (run[0] - 1) * 128:run[-1] * 128], (run[0] - bs[0]) * 128, 128 * len(run), xnames(run)))
                return segs

            ev_ctr = [0]

            def evac(out_ap, in_ap, r, w):
                k = ev_ctr[0]
                ev_ctr[0] += 1
                if k % 2 == 0:
                    S.add("act", lambda e: e.activation(out=out_ap, in_=in_ap, func=AF.Copy), r=r, w=w)
                else:
                    S.add("dve", lambda e: e.tensor_copy(out=out_ap, in_=in_ap), r=r, w=w)

            pj_ctr = [0]
            vb_ctr = [0]

            def pj_slot():
                k = pj_ctr[0] % 2
                pj_ctr[0] += 1
                return 1 + k, 0

            def proj_items(ti):
                bs = tiles[ti]
                nt = 128 * len(bs)
                qs = ti % NQ
                items = []
                need_q = any(olo <= bb <= ohi for bb in bs)

                def norm():
                    segs = xsrc(bs)
                    ps, pn = rms_sq_mean(segs, SQ, "yt", nt, 0, 0)
                    rstd_ops(RSTD[:, 0:nt], ps, pn, "rec0")
                    for ap, toff, n, names in segs:
                        for c in range(8):
                            S.add("dve", (lambda ap, toff, n, c: lambda e: e.scalar_tensor_tensor(
                                out=HT[:, c, toff:toff + n], in0=ap[:, c, :], scalar=G("mix", l, c), in1=RSTD[:, toff:toff + n],
                                op0=ALU.mult, op1=ALU.mult))(ap, toff, n, c), r=names + ["rec0", "gvec"], w=["ht"])
                items.append(norm)

                def kchunk(j):
                    def f():
                        bk, hf = pj_slot()
                        ps = PS[bk][:, hf * 256:hf * 256 + nt]
                        for c in range(8):
                            S.add("pe", (lambda c: lambda e: e.matmul(ps, lhsT=WKV[:, c, j * 128:(j + 1) * 128], rhs=HT[:, c, 0:nt], start=(c == 0), stop=(c == 7)))(c),
                                  r=["wkv", "ht"], w=pb(bk, hf))
                        s0 = slot(bs[0])
                        evac(RK[:, s0:s0 + len(bs), j, :], ps.rearrange("p (a b) -> p a b", a=len(bs)), pb(bk, hf), [f"rk{slot(bb)}" for bb in bs])
                    return f
                for j in range(KC):
                    items.append(kchunk(j))

                def vblock(bi, bb):
                    def grp(v0, vn):
                        koff = KC * 128
                        if vn > 256:
                            bk = 1 + vb_ctr[0] % 2
                            vb_ctr[0] += 1
                            ps = PS[bk][:, 0:vn]
                            pn = pb(bk)
                        else:
                            bk, hf = pj_slot()
                            ps = PS[bk][:, hf * 256:hf * 256 + vn]
                            pn = pb(bk, hf)
                        for c in range(8):
                            S.add("pe", (lambda c: lambda e: e.matmul(ps, lhsT=HT[:, c, bi * 128:(bi + 1) * 128], rhs=WKV[:, c, koff + v0:koff + v0 + vn],
                                                                      start=(c == 0), stop=(c == 7)))(c), r=["wkv", "ht"], w=pn)
                        s = slot(bb)
                        if isA:
                            evac(RV[:, s, :, 0:64], ps.rearrange("p (a b) -> p a b", a=4), pn, [f"rv{s}"])
                        else:
                            evac(RV[:, s, v0:v0 + vn], ps, pn, [f"rv{s}"])

                    def f():
                        for (v0, vn) in ([(0, 256)] if isA else [(0, 384), (384, 384)]):
                            grp(v0, vn)
                    return f
                for bi, bb in enumerate(bs):
                    items.append(vblock(bi, bb))

                def qchunk(j):
                    def f():
                        wap, wname = piece("q", j)
                        bk, hf = pj_slot()
                        ps = PS[bk][:, hf * 256:hf * 256 + nt]
                        for c in range(8):
                            S.add("pe", (lambda c: lambda e: e.matmul(ps, lhsT=wap[:, c, :], rhs=HT[:, c, 0:nt], start=(c == 0), stop=(c == 7)))(c),
                                  r=[wname, "ht"], w=pb(bk, hf))
                        evac(QT[:, qs, j, 0:nt], ps, pb(bk, hf), [f"q{qs}"])
                    return f
                if need_q:
                    for j in range(8):
                        items.append(qchunk(j))
                return items

            sk_ctr = [0]
            pv_ctr = [0]
            pf_ctr = [0]
            pbt_ctr = [0]

            def attn_items(ti):
                bs = tiles[ti]
                qs = ti % NQ
                items = []
                for bi, bb in enumerate(bs):
                    if not (olo <= bb <= ohi):
                        continue
                    qsl = slice(bi * 128, (bi + 1) * 128)
                    if not DBG.get("tok", True):
                        pass
                    elif isA:
                        stage2 = []
                        for g in range(4):
                            par, m = g % 2, g // 2
                            prt = slice(par * 64, par * 64 + 64)
                            pbi = pbt_ctr[0] % 2
                            pbt_ctr[0] += 1
                            PBg = PBT[:, pbi, :].rearrange("p (a b) -> p a b", a=3)

                            def st1(g=g, par=par, m=m, prt=prt, pbi=pbi, PBg=PBg, bb=bb, qsl=qsl):
                                for c in range(3):
                                    kbk = bb - 1 + c
                                    sk = 3 + sk_ctr[0] % 3
                                    sk_ctr[0] += 1
                                    sps = PS[sk][:, 0:384]
                                    S.add("pe", (lambda sps, kbk: lambda e: e.matmul(sps.rearrange("p (a b) -> p a b", a=3), lhsT=RK[prt, slot(kbk), m, :],
                                                                                     rhs=QT[prt, qs, m * 3:m * 3 + 3, qsl], start=True, stop=True))(sps, kbk),
                                          r=[f"rk{slot(kbk)}", f"q{qs}"], w=pb(sk))
                                    pfi = pf_ctr[0] % NPF
                                    pf_ctr[0] += 1
                                    S.add("act", (lambda sps, kbk, pfi: lambda e: e.activation(out=PF[:, pfi, :], in_=sps, func=AF.Exp, scale=0.125,
                                                                                               bias=kb[:, kbk:kbk + 1]))(sps, kbk, pfi),
                                          r=pb(sk) + ["kb"], w=[f"pf{pfi}"])
                                    S.add("dve", (lambda c, pfi: lambda e: e.tensor_tensor(out=PBg[:, c, :].rearrange("p (a b) -> p a b", a=3),
                                                                                           in0=PF[:, pfi, :].rearrange("p (a b) -> p a b", a=3),
                                                                                           in1=ET[:, 3 * g:3 * g + 3, c, :], op=ALU.mult))(c, pfi),
                                          r=[f"pf{pfi}", "et"], w=[f"pbt{pbi}"])

                            def st2(g=g, par=par, m=m, prt=prt, pbi=pbi, PBg=PBg, bb=bb, qsl=qsl):
                                ok = 6 + pv_ctr[0] % 2
                                pv_ctr[0] += 1
                                ops = PS[ok][:, 0:384]
                                for c in range(3):
                                    kbk = bb - 1 + c
                                    S.add("pe", (lambda c, kbk: lambda e: e.matmul(ops, lhsT=RV[:, slot(kbk), g, :], rhs=PBg[:, c, :], start=(c == 0), stop=(c == 2)))(c, kbk),
                                          r=[f"rv{slot(kbk)}", f"pbt{pbi}"], w=pb(ok))
                                for i in range(3):
                                    h = 3 * g + i
                                    S.add("dve", (lambda i, h: lambda e: e.tensor_scalar_add(out=REC[64:128, i * 128:(i + 1) * 128], in0=ops[64:128, i * 128:(i + 1) * 128],
                                                                                             scalar1=EXS[64:128, h:h + 1]))(i, h),
                                          r=pb(ok) + ["exs"], w=["rec1"])
                                S.add("dve", lambda e: e.reciprocal(out=REC[64:128, 0:384], in_=REC[64:128, 0:384]), r=["rec1"], w=["rec1"])
                                S.add("dve", lambda e: e.tensor_tensor(out=YT[prt, m * 3:m * 3 + 3, qsl], in0=ops[0:64, :].rearrange("p (a b) -> p a b", a=3),
                                                                       in1=REC[64:128, 0:384].rearrange("p (a b) -> p a b", a=3), op=ALU.mult),
                                      r=pb(ok) + ["rec1"], w=["yt"])
                            items.append(st1)
                            stage2.append(st2)
                            if len(stage2) > 1:
                                items.append(stage2.pop(0))
                        items.extend(stage2)
                    else:
                        if bb == 6:
                            koffs, kind = [-2, -1, 0, 1, 2, 3], 1
                        elif bb == 21:
                            koffs, kind = [-3, -2, -1, 0, 1, 2], 4
                        else:
                            koffs, kind = [-2, -1, 0, 1, 2], {7: 2, 20: 3}.get(bb, 0)
                        nk = len(koffs)
                        items.append((lambda kind: lambda: ensure_table(kind))(kind))
                        stage2 = []
                        for h in range(12):
                            par, j = h % 2, h // 2
                            prt = slice(par * 64, par * 64 + 64)
                            pbi = pbt_ctr[0] % 2
                            pbt_ctr[0] += 1
                            hp = h % 2
                            bankA = 3 + hp
                            bankB = 5 if hp == 0 else 0

                            def st1(h=h, par=par, j=j, prt=prt, pbi=pbi, bankA=bankA, bankB=bankB, bb=bb, qsl=qsl, koffs=koffs, nk=nk):
                                for i, ko in enumerate(koffs):
                                    kbk = bb + ko
                                    if i < 4:
                                        sps, pn = PS[bankA][:, i * 128:(i + 1) * 128], pb(bankA)
                                    else:
                                        sps, pn = PS[bankB][:, (i - 4) * 128:(i - 3) * 128], pb(bankB)
                                    S.add("pe", (lambda sps, kbk: lambda e: e.matmul(sps, lhsT=RK[prt, slot(kbk), j, :], rhs=QT[prt, qs, j, qsl], start=True, stop=True))(sps, kbk),
                                          r=[f"rk{slot(kbk)}", f"q{qs}"], w=pn)
                                pfi = pf_ctr[0] % NPF
                                pf_ctr[0] += 1
                                S.add("act", lambda e: e.activation(out=PF[:, pfi, 0:512], in_=PS[bankA][:, 0:512], func=AF.Exp, scale=0.125), r=pb(bankA), w=[f"pf{pfi}"])
                                nb_ = (nk - 4) * 128
                                S.add("act", lambda e: e.activation(out=PF[:, pfi, 512:512 + nb_], in_=PS[bankB][:, 0:nb_], func=AF.Exp, scale=0.125),
                                      r=pb(bankB), w=[f"pf{pfi}"])
                                S.add("dve", lambda e: e.tensor_tensor(out=PBT[:, pbi, 0:nk * 128], in0=PF[:, pfi, 0:nk * 128],
                                                                       in1=ET[:, h, 0:nk, :].rearrange("p a b -> p (a b)"), op=ALU.mult),
                                      r=[f"pf{pfi}", "et"], w=[f"pbt{pbi}"])

                            def st2(h=h, par=par, j=j, prt=prt, pbi=pbi, bb=bb, qsl=qsl, koffs=koffs, nk=nk):
                                k2 = pv_ctr[0] % 2
                                pv_ctr[0] += 1
                                ok, oh = 6 + k2, 0
                                o_ps = PS[ok][0:64, oh * 256:oh * 256 + 128]
                                s_ps = PS[ok][0:64, oh * 256 + 128:oh * 256 + 256]
                                for i, ko in enumerate(koffs):
                                    kbk = bb + ko
                                    S.add("pe", (lambda i, kbk: lambda e: e.matmul(o_ps, lhsT=RV[:, slot(kbk), h * 64:(h + 1) * 64], rhs=PBT[:, pbi, i * 128:(i + 1) * 128],
                                                                                   start=(i == 0), stop=(i == nk - 1)))(i, kbk),
                                          r=[f"rv{slot(kbk)}", f"pbt{pbi}"], w=pb(ok, oh))
                                for i in range(nk):
                                    S.add("pe", (lambda i: lambda e: e.matmul(s_ps, lhsT=ones1[:, :], rhs=PBT[:, pbi, i * 128:(i + 1) * 128], start=(i == 0), stop=(i == nk - 1)))(i),
                                          r=["ones1", f"pbt{pbi}"], w=pb(ok, oh))
                                S.add("dve", lambda e: e.tensor_scalar_max(out=REC[0:64, 0:128], in0=s_ps, scalar1=1e-30), r=pb(ok, oh), w=["rec1"])
                                S.add("dve", lambda e: e.reciprocal(out=REC[0:64, 0:128], in_=REC[0:64, 0:128]), r=["rec1"], w=["rec1"])
                                S.add("dve", lambda e: e.tensor_tensor(out=YT[prt, j, qsl], in0=o_ps, in1=REC[0:64, 0:128], op=ALU.mult), r=pb(ok, oh) + ["rec1"], w=["yt"])
                            items.append(st1)
                            stage2.append(st2)
                            if len(stage2) > 1:
                                items.append(stage2.pop(0))
                        items.extend(stage2)

                    pbm = pbt_ctr[0] % 2
                    pbt_ctr[0] += 1
                    PM = PBT[:, pbm, 0:1024]

                    def mem1(bb=bb, qsl=qsl, pbm=pbm, PM=PM):
                        for mh in range(4):
                            half, ch = mh % 2, mh // 2
                            prt = slice(half * 64, half * 64 + 64)
                            for mc in range(2):
                                bk = 3 + half
                                ci = ch * 2 + mc
                                sps = PS[bk][:, ci * 128:(ci + 1) * 128]
                                S.add("pe", (lambda sps, prt, ch, mc: lambda e: e.matmul(sps, lhsT=kmT[prt, ch, mc * 128:(mc + 1) * 128], rhs=QT[prt, qs, 6 + ch, qsl],
                                                                                         start=True, stop=True))(sps, prt, ch, mc), r=["kmT", f"q{qs}"], w=pb(bk))
                        for k2 in range(2):
                            S.add("act", (lambda k2: lambda e: e.activation(out=PM[:, k2 * 512:(k2 + 1) * 512], in_=PS[3 + k2][:, :], func=AF.Exp, scale=0.125))(k2),
                                  r=pb(3 + k2), w=[f"pbt{pbm}"])

                    def mem2(bb=bb, qsl=qsl, pbm=pbm, PM=PM):
                        ok = 6 + pv_ctr[0] % 2
                        pv_ctr[0] += 1
                        for mh in range(4):
                            for mc in range(2):
                                idx = (mh % 2) * 4 + (mh // 2) * 2 + mc
                                S.add("pe", (lambda mh, mc, idx: lambda e: e.matmul(PS[ok][:, mh * 128:(mh + 1) * 128], lhsT=vma[:, mc, mh, :], rhs=PM[:, idx * 128:(idx + 1) * 128],
                                                                                    start=(mc == 0), stop=(mc == 1)))(mh, mc, idx), r=["vma", f"pbt{pbm}"], w=pb(ok))
                        S.add("dve", lambda e: e.tensor_scalar_max(out=REC[64:128, 0:512], in0=PS[ok][64:128, :], scalar1=1e-30), r=pb(ok), w=["rec1"])
                        S.add("dve", lambda e: e.reciprocal(out=REC[64:128, 0:512], in_=REC[64:128, 0:512]), r=["rec1"], w=["rec1"])
                        for mh in range(4):
                            half, ch = mh % 2, 6 + mh // 2
                            prt = slice(half * 64, half * 64 + 64)
                            S.add("dve", (lambda mh, ch, prt: lambda e: e.tensor_tensor(out=YT[prt, ch, qsl], in0=PS[ok][0:64, mh * 128:(mh + 1) * 128],
                                                                                        in1=REC[64:128, mh * 128:(mh + 1) * 128], op=ALU.mult))(mh, ch, prt),
                                  r=pb(ok) + ["rec1"], w=["yt"])
                    if DBG.get("mem", True):
                        items.append(mem1)
                        items.append(mem2)
                return items

            def wout_items(ti):
                bs = [bb for bb in tiles[ti] if olo <= bb <= ohi]
                if not bs:
                    return []
                t0 = (bs[0] - tiles[ti][0]) * 128
                nt = 128 * len(bs)
                items = []
                for j in range(8):
                    def f(j=j):
                        wap, wname = piece("o", j)
                        bk, hf = pj_slot()
                        ps = PS[bk][:, hf * 256:hf * 256 + nt]
                        for c in range(8):
                            S.add("pe", (lambda c: lambda e: e.matmul(ps, lhsT=wap[:, c, :], rhs=YT[:, c, t0:t0 + nt], start=(c == 0), stop=(c == 7)))(c),
                                  r=[wname, "yt"], w=pb(bk, hf))
                        xs = X[:, j, (bs[0] - 1) * 128:bs[-1] * 128]
                        S.add("dve", lambda e: e.tensor_tensor(out=xs, in0=ps, in1=xs, op=ALU.add), r=pb(bk, hf) + xnames(bs), w=xnames(bs))
                    items.append(f)
                return items

            def run(items):
                for it in items:
                    it()

            def interleave(a, bl):
                if not bl:
                    run(a)
                    return
                if not a:
                    run(bl)
                    return
                ratio = len(bl) / len(a)
                acc = 0.0
                k = 0
                for it in a:
                    it()
                    acc += ratio
                    while k < len(bl) and k < int(acc + 1e-9):
                        bl[k]()
                        k += 1
                while k < len(bl):
                    bl[k]()
                    k += 1

            NT = len(tiles)
            for t in range(min(LA, NT)):
                run(proj_items(t))
            for t in range(NT):
                nxt = proj_items(t + LA) if t + LA < NT else []
                at = attn_items(t)
                if LA >= 2:
                    interleave(at, nxt)
                else:
                    run(nxt)
                    run(at)
                run(wout_items(t))


        if DBG.get('attn', True):
            phase_a()
        if not DBG.get('ffn', True):
            return

        S.barrier()
        off = 0
        HF = take([128, 8, 1024], BF16)
        SQF = take([128, 8, 1024], BF16) if False else None
        ACTB = take([128, NFF, 1024], BF16)
        SG = take([128, 2, 512], F32)
        WGU = take([128, 3, 8, 256], BF16)
        WDN = take([128, 3, NFF, 128], BF16)
        RSTDF = take([128, 1024], F32)
        assert off <= RB, (l, off)
        SQF = ACTB[:, 0:8, :]
        ftiles = []
        b = olo
        while b <= ohi:
            ftiles.append(list(range(b, min(b + 8, ohi + 1))))
            b += 8
        gu_ctr = [0]
        dn_ctr = [0]
        gset = [0]
        dbk = [0]
        def ffn_tile(bs):
            nt = 128 * len(bs)
            halves = [(h0, min(512, nt - h0)) for h0 in range(0, nt, 512)]
            xs = X[:, :, (bs[0] - 1) * 128:bs[-1] * 128]
            S.add("act", lambda e: e.activation(out=SQF[:, :, 0:nt], in_=xs, func=AF.Square), r=xnames(bs), w=["actb"])
            for hi_, (h0, hn) in enumerate(halves):
                for c in range(8):
                    S.add("pe", (lambda c, h0, hn, hi_: lambda e: e.matmul(PS[hi_][:, 0:hn], lhsT=onesn[:], rhs=SQF[:, c, h0:h0 + hn], start=(c == 0), stop=(c == 7)))(c, h0, hn, hi_),
                          r=["actb", "onesn"], w=pb(hi_))
                rstd_ops(RSTDF[:, h0:h0 + hn], PS[hi_][:, 0:hn], pb(hi_), "rstdf")
            for c in range(8):
                S.add("dve", (lambda c: lambda e: e.scalar_tensor_tensor(out=HF[:, c, 0:nt], in0=xs[:, c, :], scalar=G("ffn", l, c), in1=RSTDF[:, 0:nt],
                                                                          op0=ALU.mult, op1=ALU.mult))(c), r=xnames(bs) + ["rstdf", "gvec"], w=["hf"])
            for j in range(NFF):
                s = gu_ctr[0] % 3
                gu_ctr[0] += 1
                S.add("pool", (lambda s, j: lambda e: e.dma_start(out=WGU[:, s], in_=wgu_d[l][j]))(s, j), w=[f"wgu{s}"], dsem=f"wgu{s}")
                gs = gset[0] % 2
                gset[0] += 1
                for hi_, (h0, hn) in enumerate(halves):
                    gb, ub = gs * 4 + hi_, gs * 4 + 2 + hi_
                    for (bk, c0) in ((gb, 0), (ub, 128)):
                        for c in range(8):
                            S.add("pe", (lambda bk, c0, c, h0, hn, s: lambda e: e.matmul(PS[bk][:, 0:hn], lhsT=WGU[:, s, c, c0:c0 + 128], rhs=HF[:, c, h0:h0 + hn],
                                                                                       start=(c == 0), stop=(c == 7)))(bk, c0, c, h0, hn, s),
                                  r=[f"wgu{s}", "hf"], w=pb(bk))
                    sgi = (j * 2 + hi_) % 2
                    S.add("act", (lambda gb, h0, hn, sgi: lambda e: e.activation(out=SG[:, sgi, 0:hn], in_=PS[gb][:, 0:hn], func=AF.Silu))(gb, h0, hn, sgi),
                          r=pb(gb), w=[f"sg{sgi}"])
                    S.add("dve", (lambda ub, h0, hn, sgi, j: lambda e: e.tensor_tensor(out=ACTB[:, j, h0:h0 + hn], in0=PS[ub][:, 0:hn], in1=SG[:, sgi, 0:hn], op=ALU.mult))(ub, h0, hn, sgi, j),
                          r=pb(ub) + [f"sg{sgi}"], w=["actb"])
            for i in range(8):
                s = dn_ctr[0] % 3
                dn_ctr[0] += 1
                S.add("pool", (lambda s, i: lambda e: e.dma_start(out=WDN[:, s], in_=wdn_d[l][i]))(s, i), w=[f"wdn{s}"], dsem=f"wdn{s}")
                for (h0, hn) in halves:
                    bk = dbk[0] % 8
                    dbk[0] += 1
                    for j in range(NFF):
                        S.add("pe", (lambda bk, j, h0, hn, s: lambda e: e.matmul(PS[bk][:, 0:hn], lhsT=WDN[:, s, j, :], rhs=ACTB[:, j, h0:h0 + hn],
                                                                               start=(j == 0), stop=(j == NFF - 1)))(bk, j, h0, hn, s),
                              r=[f"wdn{s}", "actb"], w=pb(bk))
                    xd = X[:, i, (bs[0] - 1) * 128 + h0:(bs[0] - 1) * 128 + h0 + hn]
                    S.add("dve", (lambda bk, hn, xd: lambda e: e.tensor_tensor(out=xd, in0=PS[bk][:, 0:hn], in1=xd, op=ALU.add))(bk, hn, xd),
                          r=pb(bk) + xnames(bs), w=xnames(bs))

        for bs_ in ftiles:
            ffn_tile(bs_)

    for l_ in range(depth):
        emit_layer(l_)

    S.barrier()
    SQO = carve(0, [128, 8, 1024], BF16)
    RSO = carve(16384, [128, 1024], F32)
    OUTB = carve(20480, [128, 8, 1024], F32)
    def final_tile(t):
        bs = list(range(OWN_LO + 8 * t, OWN_LO + 8 * t + 8))
        xs = X[:, :, (bs[0] - 1) * 128:bs[-1] * 128]
        S.add("act", lambda e: e.activation(out=SQO, in_=xs, func=AF.Square), r=xnames(bs), w=["sqo"])
        for hi_ in range(2):
            for c in range(8):
                S.add("pe", (lambda c, hi_: lambda e: e.matmul(PS[hi_][:, :], lhsT=onesn[:], rhs=SQO[:, c, hi_ * 512:(hi_ + 1) * 512], start=(c == 0), stop=(c == 7)))(c, hi_),
                      r=["sqo", "onesn"], w=pb(hi_))
            rstd_ops(RSO[:, hi_ * 512:(hi_ + 1) * 512], PS[hi_][:, :], pb(hi_), "rso")
        for c in range(8):
            S.add("dve", (lambda c: lambda e: e.scalar_tensor_tensor(out=OUTB[:, c, :], in0=xs[:, c, :], scalar=G("fin", 0, c), in1=RSO, op0=ALU.mult, op1=ALU.mult))(c),
                  r=xnames(bs) + ["rso", "gvec"], w=["outb"])
        S.add("sp", (lambda t: lambda e: e.dma_start(out=out_d[:, :, t * 1024:(t + 1) * 1024], in_=OUTB))(t), r=["outb"], w=[f"out{t}"], dsem="out")
    for t_ in range(2):
        final_tile(t_)
    S.add("sp", lambda e: e.nop(), r=["out0", "out1"], w=["done"])

    semnames = S.finalize()
    sems = {s: es.enter_context(nc.semaphore(s.replace(":", "_"))) for s in semnames}
    with nc.Block() as block:
        @block.sync
        def _(e):
            S.emit("sp", e, sems)

        @block.gpsimd
        def _(e):
            S.emit("pool", e, sems)

        @block.scalar
        def _(e):
            S.emit("act", e, sems, spacer=lambda e_: e_.activation(out=duma[:], in_=duma[:], func=AF.Copy))

        @block.vector
        def _(e):
            S.emit("dve", e, sems, spacer=lambda e_: e_.memset(dumv[:], 0.0))

        @block.tensor
        def _(e):
            S.emit("pe", e, sems)
    es.close()
    return nc


def _fm(w):
    K, Fd = w.shape
    return np.ascontiguousarray(w.reshape(K // 128, 128, Fd).transpose(1, 0, 2))


def _pieces_cols(w, cols_list):
    return np.ascontiguousarray(np.stack([_fm(w[:, cols]) for cols in cols_list], axis=0))


def _neigh_table(rpb, ab, koffs):
    tab = np.full((12, 128, 6, 128), NEG, dtype=np.float32)
    kr = np.arange(2)[:, None, None, None]
    kc = np.arange(64)[None, :, None, None]
    qr = np.arange(2)[None, None, :, None]
    qc = np.arange(64)[None, None, None, :]
    for i, ko in enumerate(koffs):
        krow = 2 * (ab + ko) + kr
        r = 2 * ab + qr
        rs = np.clip(r - 4, 0, 256 - 8)
        cs = np.clip(qc - 8, 0, 64 - 16)
        valid = (krow >= rs) & (krow < rs + 8) & (krow >= 0) & (krow < 256) & (kc >= cs) & (kc < cs + 16)
        dr = np.clip(krow - r + 7, 0, 14)
        dc = np.clip(kc - qc, -15, 15) + 15
        valid, dr, dc = np.broadcast_arrays(valid, dr, dc)
        g = rpb[:, dr, dc]
        g = np.where(valid[None], g, np.float32(NEG)).reshape(12, 128, 128)
        tab[:, :, i, :] = g
    return tab


def _prep(inputs, depth):
    f = lambda a: np.asarray(a, dtype=np.float32)
    x = f(inputs["x"])[0]
    mem = f(inputs["mem"])[0]
    common = {}
    common["memT"] = _fm(mem.T.copy()).copy()
    gv = np.concatenate([f(inputs["norm_mix"]).reshape(4, 8, 128).transpose(2, 0, 1).reshape(128, 32),
                         f(inputs["norm_ffn"]).reshape(4, 8, 128).transpose(2, 0, 1).reshape(128, 32),
                         f(inputs["norm_mem"]).reshape(8, 128).T, f(inputs["norm_final"]).reshape(8, 128).T], axis=1)
    common["gvec"] = np.ascontiguousarray(gv)
    common["sinkb"] = np.ascontiguousarray(np.broadcast_to(f(inputs["sink_a"]).reshape(1, 24), (128, 24)))
    k = np.arange(128)[:, None, None]
    c = np.arange(3)[None, :, None]
    q = np.arange(128)[None, None, :]
    dist = np.abs(q - ((c - 1) * 128 + k)).astype(np.float32)
    common["negdA"] = np.ascontiguousarray(np.where(dist <= 128, -dist, np.float32(-1e6)).astype(np.float32))
    w_in_a, w_in_b = f(inputs["w_in_a"]), f(inputs["w_in_b"])
    w_out, w_mkv = f(inputs["w_out"]), f(inputs["w_mem_kv"])
    w_gu, w_dn = f(inputs["w_gate_up"]), f(inputs["w_down"])
    rpb = f(inputs["rpb_b"])
    ar = np.arange
    for l in range(depth):
        li = l // 2
        if LAYERS[l][0] == "A":
            w = w_in_a[li]
            common[f"wkv{l}"] = _fm(w[:, 768:1280])
            qcols, orows = [], []
            for m in range(2):
                for i in range(3):
                    h0, h1 = 3 * (2 * m) + i, 3 * (2 * m + 1) + i
                    idx = np.concatenate([h0 * 64 + ar(64), h1 * 64 + ar(64)])
                    qcols.append(idx)
                    orows.append(idx)
            qcols += [1280 + ar(128), 1408 + ar(128)]
            orows += [768 + ar(128), 896 + ar(128)]
        else:
            w = w_in_b[li]
            common[f"wkv{l}"] = _fm(w[:, 768:2304])
            qcols = [j * 128 + ar(128) for j in range(6)] + [2304 + ar(128), 2432 + ar(128)]
            orows = [j * 128 + ar(128) for j in range(8)]
        common[f"wqp{l}"] = _pieces_cols(w, qcols)
        wo = w_out[l][np.concatenate(orows), :]
        common[f"wop{l}"] = _pieces_cols(wo, [j * 128 + ar(128) for j in range(8)])
        common[f"wmk{l}"] = _fm(w_mkv[l])
        gu = w_gu[l]
        common[f"wgu{l}"] = np.ascontiguousarray(np.stack(
            [_fm(np.concatenate([gu[:, j * 128:(j + 1) * 128], gu[:, DFF + j * 128:DFF + (j + 1) * 128]], axis=1)) for j in range(NFF)], axis=0))
        dn = w_dn[l]
        common[f"wdn{l}"] = np.ascontiguousarray(dn.reshape(NFF, 128, 8, 128).transpose(2, 1, 0, 3))
    in_maps = []
    for core in range(NCORES):
        m = dict(common)
        ws = core * TOK_CORE - HALO_TOK
        xw = np.zeros((128, 8, NBLK * 128), dtype=np.float32)
        t0, t1 = max(0, ws), min(SEQ, ws + NBLK * 128)
        xw[:, :, t0 - ws:t1 - ws] = x[t0:t1].T.reshape(8, 128, t1 - t0).transpose(1, 0, 2)
        m["xw"] = xw
        kbv = np.zeros((128, NBLK), dtype=np.float32)
        for b in range(NBLK):
            tb = ws + 128 * b
            if tb < 0 or tb >= SEQ:
                kbv[:, b] = NEG
        m["kb"] = kbv
        for l in range(depth):
            if LAYERS[l][0] != "B":
                continue
            r = rpb[l // 2]
            ab0 = core * 16 - OWN_LO
            tabs = [_neigh_table(r, 40, [-2, -1, 0, 1, 2]),
                    _neigh_table(r, ab0 + 6, [-2, -1, 0, 1, 2, 3]),
                    _neigh_table(r, ab0 + 7, [-2, -1, 0, 1, 2]),
                    _neigh_table(r, ab0 + 20, [-2, -1, 0, 1, 2]),
                    _neigh_table(r, ab0 + 21, [-3, -2, -1, 0, 1, 2])]
            m[f"tab{l}"] = np.ascontiguousarray(np.stack(tabs, axis=0).reshape(5, 12, 128, 768))
        in_maps.append(m)
    return in_maps


_NC_CACHE = {}


def kernel(**inputs):
    depth = 4
    if depth not in _NC_CACHE:
        _NC_CACHE[depth] = build_program(depth)
    nc = _NC_CACHE[depth]
    in_maps = _prep(inputs, depth)
    res = run_bass_kernel_spmd(nc, in_maps, core_ids=list(range(NCORES)))
    outs = []
    for core in range(NCORES):
        o = np.asarray(res.results[core]["out"], dtype=np.float32)
        outs.append(o.transpose(2, 1, 0).reshape(TOK_CORE, D))
    return np.concatenate(outs, axis=0)[None].astype(np.float32)
```
